# Optimizing a Trainium2 kernel written in Bass

```python
import jax, jax.numpy as jnp
from jax import lax
import numpy as np

D_MODEL = 2048
BATCH = 1
SEQ = 8192
DEPTH = 1

CHUNK = 64
SGU_BLOCK = 128
D_A = 2048
G_A = 8
DG_A = D_A // G_A
D_B = 2048
CONV_W = 31
D_FF = 5632
PLE_DIM = 256
IN_COLS = 2 * D_A + 2 * D_B + 2 * D_MODEL
EPS = 1e-6

kernel_name = "hybrid_gmlp_conformer_conv_macaron_block"


def rmsnorm(x, g):
    xf = x.astype(jnp.float32)
    y = xf * lax.rsqrt(jnp.mean(xf * xf, axis=-1, keepdims=True) + EPS)
    return (y * g.astype(jnp.float32)).astype(x.dtype)


def layernorm(x, g, b):
    xf = x.astype(jnp.float32)
    mu = jnp.mean(xf, axis=-1, keepdims=True)
    var = jnp.mean(jnp.square(xf - mu), axis=-1, keepdims=True)
    y = (xf - mu) * lax.rsqrt(var + EPS)
    return (y * g.astype(jnp.float32) + b.astype(jnp.float32)).astype(x.dtype)


def swiglu(x, w_gate, w_up, w_down):
    return (jax.nn.silu(x @ w_gate) * (x @ w_up)) @ w_down


def spatial_gating(u, v, ln_g, ln_b, w_s, b_s):
    bsz, seq, _ = v.shape
    nb = seq // SGU_BLOCK
    v = layernorm(v, ln_g, ln_b)
    idx = jnp.arange(SGU_BLOCK)
    allowed = (idx[None, :] // CHUNK) <= (idx[:, None] // CHUNK)
    w_m = jnp.where(allowed[None], w_s, jnp.zeros_like(w_s))
    vb = v.reshape(bsz, nb, SGU_BLOCK, G_A, DG_A)
    s = jnp.einsum('gij,bnjgc->bnigc', w_m, vb) + b_s.T[None, None, :, :, None]
    return u * s.reshape(bsz, seq, D_A)


def conformer_conv(a, b, dw_w, dw_b, ln_g, ln_b):
    z = a * jax.nn.sigmoid(b)
    y = lax.conv_general_dilated(
        z, dw_w[:, None, :].astype(z.dtype), window_strides=(1,),
        padding=[(CONV_W - 1, 0)], dimension_numbers=('NWC', 'WIO', 'NWC'),
        feature_group_count=D_B) + dw_b
    return jax.nn.silu(layernorm(y, ln_g, ln_b))


def setup_inputs(seed: int = 0) -> dict:
    key = jax.random.key(seed)
    ks = iter(jax.random.split(key, 40))
    L = DEPTH

    def w(shape, fan_in):
        return jax.random.normal(next(ks), shape, jnp.float32) * fan_in ** -0.5

    def gain(shape):
        return 1.0 + 0.05 * jax.random.normal(next(ks), shape, jnp.float32)

    def bias(shape):
        return 0.02 * jax.random.normal(next(ks), shape, jnp.float32)

    return {
        "x": jax.random.normal(next(ks), (BATCH, SEQ, D_MODEL), jnp.float32),
        "p": jax.random.normal(next(ks), (DEPTH, BATCH, SEQ, PLE_DIM), jnp.float32),
        "ffn1_norm": gain((L, D_MODEL)),
        "ffn1_w_gate": w((L, D_MODEL, D_FF), D_MODEL),
        "ffn1_w_up": w((L, D_MODEL, D_FF), D_MODEL),
        "ffn1_w_down": w((L, D_FF, D_MODEL), D_FF),
        "mix_norm": gain((L, D_MODEL)),
        "w_in": w((L, D_MODEL, IN_COLS), D_MODEL),
        "sgu_ln_g": gain((L, D_A)),
        "sgu_ln_b": bias((L, D_A)),
        "sgu_w": w((L, G_A, SGU_BLOCK, SGU_BLOCK), SGU_BLOCK) * 0.5,
        "sgu_b": 1.0 + 0.05 * jax.random.normal(next(ks), (L, G_A, SGU_BLOCK), jnp.float32),
        "w_a_proj": w((L, D_A, D_MODEL), D_A),
        "dw_w": w((L, CONV_W, D_B), CONV_W),
        "dw_b": bias((L, D_B)),
        "conv_ln_g": gain((L, D_B)),
        "conv_ln_b": bias((L, D_B)),
        "w_b_proj": w((L, D_B, D_MODEL), D_B),
        "w_out": w((L, D_MODEL, D_MODEL), D_MODEL),
        "ffn2_norm": gain((L, D_MODEL)),
        "ffn2_w_gate": w((L, D_MODEL, D_FF), D_MODEL),
        "ffn2_w_up": w((L, D_MODEL, D_FF), D_MODEL),
        "ffn2_w_down": w((L, D_FF, D_MODEL), D_FF),
        "ple_norm": gain((L, D_MODEL)),
        "w_ple_gate": w((L, D_MODEL, D_MODEL), D_MODEL),
        "w_ple_proj": w((L, PLE_DIM, D_MODEL), PLE_DIM),
        "final_norm": gain((D_MODEL,)),
    }


def reference(x, p, ffn1_norm, ffn1_w_gate, ffn1_w_up, ffn1_w_down, mix_norm, w_in,
              sgu_ln_g, sgu_ln_b, sgu_w, sgu_b, w_a_proj, dw_w, dw_b, conv_ln_g,
              conv_ln_b, w_b_proj, w_out, ffn2_norm, ffn2_w_gate, ffn2_w_up,
              ffn2_w_down, ple_norm, w_ple_gate, w_ple_proj, final_norm):
    h = x
    for i in range(DEPTH):
        h = h + 0.5 * swiglu(rmsnorm(h, ffn1_norm[i]), ffn1_w_gate[i], ffn1_w_up[i], ffn1_w_down[i])

        n = rmsnorm(h, mix_norm[i])
        z = n @ w_in[i]
        o1 = D_A
        o2 = o1 + D_A
        o3 = o2 + D_B
        o4 = o3 + D_B
        o5 = o4 + D_MODEL
        u_a, v_a = z[..., :o1], z[..., o1:o2]
        glu_a, glu_b = z[..., o2:o3], z[..., o3:o4]
        gate_a, gate_b = z[..., o4:o5], z[..., o5:]

        y_a = spatial_gating(u_a, v_a, sgu_ln_g[i], sgu_ln_b[i], sgu_w[i], sgu_b[i]) @ w_a_proj[i]
        y_b = conformer_conv(glu_a, glu_b, dw_w[i], dw_b[i], conv_ln_g[i], conv_ln_b[i]) @ w_b_proj[i]

        m = jax.nn.sigmoid(gate_a) * y_a + jax.nn.sigmoid(gate_b) * y_b
        h = h + m @ w_out[i]

        h = h + 0.5 * swiglu(rmsnorm(h, ffn2_norm[i]), ffn2_w_gate[i], ffn2_w_up[i], ffn2_w_down[i])

        g = jax.nn.sigmoid(rmsnorm(h, ple_norm[i]) @ w_ple_gate[i])
        h = h + g * (p[i] @ w_ple_proj[i])
    return rmsnorm(h, final_norm)
```

```python
import numpy as np
from contextlib import ExitStack
import concourse.bass as bass
import concourse.mybir as mybir
from concourse.bass_utils import run_bass_kernel_spmd

F32 = mybir.dt.float32
BF16 = mybir.dt.bfloat16
ALU = mybir.AluOpType
AF = mybir.ActivationFunctionType

D = 2048
FF = 5632
T = 1024
TN = 512
HN = 32
NDC = 16
NFC = 44
NCORES = 8
EPS = 1e-6
CW = 31
NS = 3
V_F1, V_MIX, V_SG, V_SB, V_DWB, V_CG, V_CB, V_F2, V_PLE, V_FIN = range(10)
NV = 10


class Prog:
    ENGS = ("pe", "act", "dve", "pool", "sp")

    def __init__(self):
        self.ops = {e: [] for e in self.ENGS}
        self.cnt = {}
        self.waited = {e: {} for e in self.ENGS}
        self.lastw = {}
        self.readers = {}

    def op(self, eng, fn, reads=(), writes=(), sem=None, inc=1):
        need = {}

        def add(s, v):
            if v > need.get(s, 0):
                need[s] = v
        for k in reads:
            w = self.lastw.get(k)
            if w:
                add(*w)
        for k in writes:
            w = self.lastw.get(k)
            if w:
                add(*w)
            for s, v in self.readers.get(k, {}).items():
                add(s, v)
        waits = []
        for s, v in need.items():
            if eng == "pe" and s == "pe":
                continue
            if self.waited[eng].get(s, 0) >= v:
                continue
            self.waited[eng][s] = v
            waits.append((s, v))
        s = sem or eng
        self.cnt[s] = self.cnt.get(s, 0) + inc
        v = self.cnt[s]
        self.ops[eng].append((waits, fn, s, inc))
        for k in reads:
            self.readers.setdefault(k, {})[s] = v
        for k in writes:
            self.lastw[k] = (s, v)
            self.readers[k] = {}
        return v

    def wait_all(self, eng, sems):
        waits = [(s, self.cnt[s]) for s in sems if self.cnt.get(s, 0) > 0]
        self.ops[eng].append((waits, None, None, 0))


def build_nc():
    nc = bass.Bass("TRN2", target_bir_lowering=False)
    P = Prog()

    def din(name, shape):
        return nc.dram_tensor(name, shape, F32, kind="ExternalInput").ap()

    xT = din("xT", [D, T])
    xh = din("xh", [D, HN])
    pT = din("pT", [256, T])
    vecs_d = din("vecs", [128, NV * NDC])
    dww_d = din("dww", [128, NDC * CW])
    wsT_d = din("wsT", [128, 8 * 128])
    bsB_d = din("bsB", [128, 8 * 128])
    hmask_d = din("hmask", [128, 1])
    Wd_ = {}
    for nm, shp in (("f1g", [D, FF]), ("f1u", [D, FF]), ("f1d", [FF, D]), ("win", [D, 6 * D]),
                    ("wa", [D, D]), ("wb", [D, D]), ("wo", [D, D]),
                    ("f2g", [D, FF]), ("f2u", [D, FF]), ("f2d", [FF, D]),
                    ("wpg", [D, D]), ("wpp", [256, D])):
        Wd_[nm] = din(nm, shp).rearrange("(k p) n -> p k n", p=128)
    oT = nc.dram_tensor("oT", [D, T], F32, kind="ExternalOutput").ap()
    oTv = oT.rearrange("(k p) n -> p k n", p=128)
    xTv = xT.rearrange("(k p) n -> p k n", p=128)
    xhv = xh.rearrange("(k p) n -> p k n", p=128)
    pTv = pT.rearrange("(k p) n -> p k n", p=128)

    st = ExitStack()
    sb = lambda name, shape, dt: st.enter_context(nc.sbuf_tensor(name, shape, dt))
    h = sb("h", [128, NDC, TN], F32)
    hh = sb("hh", [128, NDC, HN], F32)
    ztail = sb("ztail", [128, NDC, HN], BF16)
    ring = sb("ring", [128, NS, 8192], BF16)
    AW = 80 * 256 + 704
    arena = sb("arena", [128, AW], F32)
    tmp = sb("tmp", [128, 4, TN], F32)
    tmph = sb("tmph", [128, 4, HN], F32)
    sq = sb("sq", [128, 2, TN], BF16)
    ybf = sb("ybf", [128, 2, TN], BF16)
    zt = sb("zt", [128, 2, HN + TN], BF16)
    ident = sb("ident", [128, 128], BF16)
    cacc = sb("cacc", [128, TN], F32)
    NDG = 16
    dg = sb("dg", [128, NDG, 128], BF16)
    rstd = sb("rstd", [128, TN], F32)
    mu = sb("mu", [128, TN], F32)
    rstdh = sb("rstdh", [128, HN], F32)
    xnh = sb("xnh", [128, NDC, HN], BF16)
    ones1 = sb("ones1", [128, 128], BF16)
    onesD = sb("onesD", [128, 128], BF16)
    vecs = sb("vecs_sb", [128, NV, NDC], F32)
    dww = sb("dww_sb", [128, NDC, CW], F32)
    wmT = sb("wmT", [128, 8, 128], BF16)
    Cc = sb("Cc", [128, NDC, 128], F32)
    pTb = sb("pTb", [128, 2, TN], BF16)
    hmask = sb("hmask_sb", [128, 1], F32)
    vstats = sb("vstats", [128, 4, 4, 6], F32)
    vmv = sb("vmv", [128, 4, 2], F32)
    vrs = sb("vrs", [128, 4], F32)
    vnm = sb("vnm", [128, 4], F32)
    ps = st.enter_context(nc.psum_tensor("ps", [128, 8, 512], F32))

    def a_bf(lo, hi, n):
        return arena[:, lo:hi].bitcast(BF16).rearrange("p (k n) -> p k n", n=n)

    def a_f32(lo, hi, n):
        return arena[:, lo:hi].rearrange("p (k n) -> p k n", n=n)
    K = 256
    xn = a_bf(0, 16 * K, TN)
    gbuf = a_bf(16 * K, 60 * K, TN)
    R1 = a_f32(16 * K, 48 * K, TN)
    vtm = arena[:, 16 * K:48 * K].rearrange("p (t c) -> p t c", c=D)
    R2 = a_bf(48 * K, 64 * K, TN)
    vhat = arena[:, 48 * K:64 * K].bitcast(BF16).rearrange("p (t c) -> p t c", c=D)
    ya = a_bf(64 * K, 80 * K, TN)
    gh = a_bf(80 * K, 80 * K + 704, HN)
    bsB = arena[:, 16 * K:20 * K].rearrange("p (g i) -> p g i", i=128)

    def gkey(f):
        return ("R1", f // 2) if f < 32 else ("R2", f - 32)

    cnt = {"bank": 0, "tmp": 0, "tmph": 0, "sq": 0, "w": 0, "ybf": 0, "zt": 0, "o": 0, "dg": 0}

    def bank():
        b = cnt["bank"] % 6
        cnt["bank"] += 1
        return b

    def rot(name, n):
        i = cnt[name] % n
        cnt[name] += 1
        return i

    def wtile(wname, k0, k1, c0, c1):
        s = rot("w", NS)
        nk, ncol = k1 - k0, c1 - c0
        view = ring[:, s, 0:nk * ncol].rearrange("p (k n) -> p k n", n=ncol)
        src = Wd_[wname][:, k0:k1, c0:c1]
        P.op("pool", lambda e, view=view, src=src: e.dma_start(out=view, in_=src),
             writes=[("w", s)], sem="w%d" % s, inc=16)
        return view, ("w", s)

    def wtile_pair(wname, ca, cb, ncol):
        s = rot("w", NS)
        view = ring[:, s, 0:NDC * 2 * ncol].rearrange("p (k n) -> p k n", n=2 * ncol)
        srca = Wd_[wname][:, 0:NDC, ca:ca + ncol]
        srcb = Wd_[wname][:, 0:NDC, cb:cb + ncol]
        P.op("pool", lambda e: e.dma_start(out=view[:, :, 0:ncol], in_=srca),
             writes=[("w", s)], sem="w%d" % s, inc=16)
        P.op("pool", lambda e: e.dma_start(out=view[:, :, ncol:2 * ncol], in_=srcb),
             sem="w%d" % s, inc=16)
        P.lastw[("w", s)] = ("w%d" % s, P.cnt["w%d" % s])
        return view, ("w", s)

    def mm_group(out_ap, pairs, reads, writes, first=True, last=True, force_start=None):
        def fn(e):
            ins = None
            n = len(pairs)
            for i, (l, r) in enumerate(pairs):
                stt = (first and i == 0) if force_start is None else (force_start and i == 0)
                ins = e.matmul(out_ap, l, r, start=stt, stop=(last and i == n - 1),
                               skip_group_check=True)
            return ins
        P.op("pe", fn, reads=reads, writes=writes)

    def mm_group_split(out_ap, pairs, pair_keys, common_reads, writes):
        n = len(pairs)
        for i, ((l, r), pk) in enumerate(zip(pairs, pair_keys)):
            P.op("pe", lambda e, l=l, r=r, i=i: e.matmul(out_ap, l, r, start=(i == 0), stop=(i == n - 1),
                                                         skip_group_check=True),
                 reads=list(common_reads) + [pk], writes=writes)

    class Tl:
        pass

    main = Tl()
    main.n = TN
    main.h = lambda dc: h[:, dc, :]
    main.hk = lambda dc: ("h", dc)
    main.xn = lambda dc: xn[:, dc, :]
    main.xnk = lambda dc: ("xn", dc)
    main.g = lambda f: gbuf[:, f, :]
    main.gk = gkey
    main.rstd = rstd[:, :]
    main.rstdk = ("rstd",)
    main.sbank = 6
    main.halo = False
    halo = Tl()
    halo.n = HN
    halo.h = lambda dc: hh[:, dc, :]
    halo.hk = lambda dc: ("hh", dc)
    halo.xn = lambda dc: xnh[:, dc, :]
    halo.xnk = lambda dc: ("xnh", dc)
    halo.g = lambda f: gh[:, f, :]
    halo.gk = lambda f: ("gh", f)
    halo.rstd = rstdh[:, :]
    halo.rstdk = ("rstdh",)
    halo.sbank = 7
    halo.halo = True

    def tmp_alloc(tl):
        if tl.halo:
            i = rot("tmph", 4)
            return tmph[:, i, :], ("tmph", i)
        i = rot("tmp", 4)
        return tmp[:, i, :], ("tmp", i)

    def pbank(tl):
        b = bank()
        return ps[:, b, 0:tl.n], ("ps", b)

    def rms_stat_chunk(tl, src, srck, dc, pre=None):
        sbk = ("ps", tl.sbank)
        sbv = ps[:, tl.sbank, 0:tl.n]
        if pre is not None:
            pre(dc)
        i = rot("sq", 2)
        sqv = sq[:, i, 0:tl.n]
        P.op("act", lambda e, o=sqv, a=src(dc): e.activation(o, a, AF.Square),
             reads=[srck(dc)], writes=[("sq", i)])
        P.op("pe", lambda e, o=sbv, r=sqv, dc=dc: e.matmul(o, onesD[:, :], r, start=(dc == 0),
                                                          stop=(dc == NDC - 1), skip_group_check=True),
             reads=[("sq", i), ("onesD",)], writes=[sbk])

    def rms_finish(tl, dest=None, destk=None):
        sbk = ("ps", tl.sbank)
        sbv = ps[:, tl.sbank, 0:tl.n]
        dv = tl.rstd if dest is None else dest
        dk = tl.rstdk if destk is None else destk
        P.op("act", lambda e, o=dv, a=sbv: e.activation(o, a, AF.Sqrt, bias=EPS),
             reads=[sbk], writes=[dk])
        P.op("dve", lambda e, o=dv: e.reciprocal(o, o), reads=[dk], writes=[dk])

    def rms_stats(tl, src, srck, pre=None):
        for dc in range(NDC):
            rms_stat_chunk(tl, src, srck, dc, pre)
        rms_finish(tl)

    def rmsnorm(tl, vi):
        rms_stats(tl, tl.h, tl.hk)
        for dc in range(NDC):
            P.op("dve", lambda e, o=tl.xn(dc), a=tl.h(dc), s=vecs[:, vi, dc:dc + 1], r=tl.rstd:
                 e.scalar_tensor_tensor(o, a, s, r, ALU.mult, ALU.mult),
                 reads=[tl.hk(dc), tl.rstdk, ("vecs",)], writes=[tl.xnk(dc)])

    def rmsnorm_raw(tl, vi):
        def pre(dc):
            P.op("dve", lambda e, o=tl.xn(dc), a=tl.h(dc), s=vecs[:, vi, dc:dc + 1]:
                 e.tensor_scalar(o, a, s, None, ALU.mult),
                 reads=[tl.hk(dc), ("vecs",)], writes=[tl.xnk(dc)])
        rms_stats(tl, tl.h, tl.hk, pre=pre)

    def ffn(tiles, vi, wg, wu, wd, after_norm=None):
        for tl in tiles:
            rmsnorm_raw(tl, vi)
        if after_norm is not None:
            after_norm()
        for fq in range(NFC // 4):
            c0 = fq * 512
            sgs = {}
            wt, wk = wtile(wg, 0, NDC, c0, c0 + 512)
            for fl in range(4):
                for ti, tl in enumerate(tiles):
                    bv, bk = pbank(tl)
                    mm_group(bv, [(wt[:, k, fl * 128:(fl + 1) * 128], tl.xn(k)) for k in range(NDC)],
                             reads=[wk] + [tl.xnk(k) for k in range(NDC)], writes=[bk])
                    tv, tk = tmp_alloc(tl)
                    P.op("dve", lambda e, o=tv, a=bv, r=tl.rstd: e.tensor_tensor(o, a, r, ALU.mult),
                         reads=[bk, tl.rstdk], writes=[tk])
                    P.op("act", lambda e, o=tv: e.activation(o, o, AF.Silu), reads=[tk], writes=[tk])
                    P.op("dve", lambda e, o=tv, r=tl.rstd: e.tensor_tensor(o, o, r, ALU.mult),
                         reads=[tk, tl.rstdk], writes=[tk])
                    sgs[(fl, ti)] = (tv, tk)
            wt, wk = wtile(wu, 0, NDC, c0, c0 + 512)
            for fl in range(4):
                f = fq * 4 + fl
                for ti, tl in enumerate(tiles):
                    bv, bk = pbank(tl)
                    mm_group(bv, [(wt[:, k, fl * 128:(fl + 1) * 128], tl.xn(k)) for k in range(NDC)],
                             reads=[wk] + [tl.xnk(k) for k in range(NDC)], writes=[bk])
                    tv, tk = sgs[(fl, ti)]
                    P.op("dve", lambda e, o=tl.g(f), a=tv, b=bv: e.tensor_tensor(o, a, b, ALU.mult),
                         reads=[tk, bk], writes=[tl.gk(f)])
        kgs = [(0, 16), (16, 32), (32, 44)]
        for cq in range(4):
            c0 = cq * 512
            banks = {}
            hb = bank() if len(tiles) > 1 else None
            for gi, (k0, k1) in enumerate(kgs):
                wt, wk = wtile(wd, k0, k1, c0, c0 + 512)
                for cl in range(4):
                    for ti, tl in enumerate(tiles):
                        if ti == 0:
                            if gi == 0:
                                banks[cl] = pbank(tl)
                            bv, bk = banks[cl]
                            fs = None
                        else:
                            bv, bk = ps[:, hb, cl * HN:(cl + 1) * HN], ("ps", hb)
                            fs = (gi == 0 and cl == 0)
                        mm_group(bv, [(wt[:, k - k0, cl * 128:(cl + 1) * 128], tl.g(k)) for k in range(k0, k1)],
                                 reads=[wk] + [tl.gk(k) for k in range(k0, k1)], writes=[bk],
                                 first=(gi == 0), last=(gi == 2), force_start=fs)
            for cl in range(4):
                dc = cq * 4 + cl
                for ti, tl in enumerate(tiles):
                    if ti == 0:
                        bv, bk = banks[cl]
                    else:
                        bv, bk = ps[:, hb, cl * HN:(cl + 1) * HN], ("ps", hb)
                    P.op("dve", lambda e, o=tl.h(dc), b=bv: e.scalar_tensor_tensor(o, b, 0.5, o, ALU.mult, ALU.add),
                         reads=[bk, tl.hk(dc)], writes=[tl.hk(dc)])

    def mix(p):
        tiles = [main, halo] if p == 0 else [main]
        for tl in tiles:
            rmsnorm(tl, V_MIX)
        for cb in range(4):
            wt, wk = wtile("win", 0, NDC, D + cb * 512, D + (cb + 1) * 512)
            for tb in range(4):
                b = bank()
                bv, bk = ps[:, b, :], ("ps", b)
                if cb == 0 and tb == 0:
                    mm_group_split(bv, [(xn[:, k, tb * 128:(tb + 1) * 128], wt[:, k, :]) for k in range(NDC)],
                                   [("xn", k) for k in range(NDC)], [wk], [bk])
                else:
                    mm_group(bv, [(xn[:, k, tb * 128:(tb + 1) * 128], wt[:, k, :]) for k in range(NDC)],
                             reads=[wk] + [("xn", k) for k in range(NDC)], writes=[bk])
                gran = ("R1", tb * 4 + cb)
                P.op("act", lambda e, o=vtm[:, tb, cb * 512:(cb + 1) * 512], a=bv: e.activation(o, a, AF.Copy),
                     reads=[bk], writes=[gran])
                P.op("dve", lambda e, o=vstats[:, tb, cb, :], a=vtm[:, tb, cb * 512:(cb + 1) * 512]: e.bn_stats(o, a),
                     reads=[gran], writes=[("vstats", tb, cb)])
        for tb in range(4):
            P.op("dve", lambda e, o=vmv[:, tb, :], a=vstats[:, tb, :, :].rearrange("p a b -> p (a b)"): e.bn_aggr(o, a),
                 reads=[("vstats", tb, cb) for cb in range(4)], writes=[("vmv", tb)])
            P.op("act", lambda e, o=vrs[:, tb:tb + 1], a=vmv[:, tb, 1:2]: e.activation(o, a, AF.Sqrt, bias=EPS),
                 reads=[("vmv", tb)], writes=[("vrs", tb)])
            P.op("dve", lambda e, o=vrs[:, tb:tb + 1]: e.reciprocal(o, o), reads=[("vrs", tb)], writes=[("vrs", tb)])
            P.op("dve", lambda e, o=vnm[:, tb:tb + 1], a=vmv[:, tb, 0:1], s=vrs[:, tb:tb + 1]:
                 e.tensor_scalar(o, a, s, -1.0, ALU.mult, ALU.mult),
                 reads=[("vmv", tb), ("vrs", tb)], writes=[("vnm", tb)])
            P.op("act", lambda e, o=vhat[:, tb, :], a=vtm[:, tb, :], s=vrs[:, tb:tb + 1], bb=vnm[:, tb:tb + 1]:
                 e.activation(o, a, AF.Identity, bias=bb, scale=s),
                 reads=[("R1", tb * 4 + i) for i in range(4)] + [("vrs", tb), ("vnm", tb)],
                 writes=[("R2", tb * 4 + i) for i in range(4)])
        for uq in range(4):
            wt, wk = wtile("win", 0, NDC, uq * 512, (uq + 1) * 512)
            for cl in range(4):
                c = uq * 4 + cl
                g = c // 2
                b = bank()
                sv, sk = ps[:, b, :], ("ps", b)
                for tb in range(4):
                    mm_group(ps[:, b, tb * 128:(tb + 1) * 128],
                             [(vhat[:, tb, c * 128:(c + 1) * 128], wmT[:, g, :])],
                             reads=[("R2", tb * 4 + c // 4), ("wmT",)], writes=[sk],
                             force_start=(tb == 0))
                uv, uk = pbank(main)
                mm_group(uv, [(wt[:, k, cl * 128:(cl + 1) * 128], xn[:, k, :]) for k in range(NDC)],
                         reads=[wk] + [("xn", k) for k in range(NDC)], writes=[uk])
                tv, tk = tmp_alloc(main)
                P.op("dve", lambda e, o=tv.rearrange("p (t i) -> p t i", i=128),
                     a=sv.rearrange("p (t i) -> p t i", i=128), s=vecs[:, V_SG, c:c + 1],
                     cc=Cc[:, c:c + 1, :].to_broadcast([128, 4, 128]):
                     e.scalar_tensor_tensor(o, a, s, cc, ALU.mult, ALU.add),
                     reads=[sk, ("Cc",), ("vecs",)], writes=[tk])
                P.op("dve", lambda e, o=ya[:, c, :], a=tv, bb=uv: e.tensor_tensor(o, a, bb, ALU.mult),
                     reads=[tk, uk], writes=[("ya", c)])
        mb, qb = ("ps", 6), ("ps", 7)
        wts = {}

        def glu_chunk(c):
            hq, cl = divmod(c, 2)
            if cl == 0:
                wts["ab"] = wtile_pair("win", 2 * D + hq * 256, 3 * D + hq * 256, 256)
            wab, wkab = wts["ab"]
            wta, wka = wab[:, :, 0:256], wkab
            wtb, wkb = wab[:, :, 256:512], wkab
            zi = rot("zt", 2)
            zk = ("zt", zi)
            res = {}
            abanks = {}
            for ti, tl in enumerate(tiles):
                av, ak = pbank(tl)
                mm_group(av, [(wta[:, k, cl * 128:(cl + 1) * 128], tl.xn(k)) for k in range(NDC)],
                         reads=[wka] + [tl.xnk(k) for k in range(NDC)], writes=[ak])
                abanks[ti] = (av, ak)
            for ti, tl in enumerate(tiles):
                av, ak = abanks[ti]
                bv, bk = pbank(tl)
                mm_group(bv, [(wtb[:, k, cl * 128:(cl + 1) * 128], tl.xn(k)) for k in range(NDC)],
                         reads=[wkb] + [tl.xnk(k) for k in range(NDC)], writes=[bk])
                tv, tk = tmp_alloc(tl)
                P.op("act", lambda e, o=tv, a=bv: e.activation(o, a, AF.Sigmoid), reads=[bk], writes=[tk])
                res[ti] = (av, ak, tv, tk)
            av, ak, tv, tk = res[0]
            if p == 0:
                hv, hk_, htv, htk = res[1]
                P.op("dve", lambda e, o=zt[:, zi, 0:HN], a=hv, s=hmask[:, 0:1], t=htv:
                     e.scalar_tensor_tensor(o, a, s, t, ALU.mult, ALU.mult),
                     reads=[hk_, htk, ("hmask",)], writes=[zk])
            else:
                P.op("act", lambda e, o=zt[:, zi, 0:HN], a=ztail[:, c, :]: e.activation(o, a, AF.Copy),
                     reads=[("ztail", c)], writes=[zk])
            P.op("dve", lambda e, o=zt[:, zi, HN:HN + TN], a=av, t=tv: e.tensor_tensor(o, a, t, ALU.mult),
                 reads=[ak, tk], writes=[zk])
            if p == 0:
                P.op("act", lambda e, o=ztail[:, c, :], a=zt[:, zi, TN:TN + HN]: e.activation(o, a, AF.Copy),
                     reads=[zk], writes=[("ztail", c)])
            return zi, zk

        DT = 12
        PEK = list(range(DT, CW))
        NB = (len(PEK) + 7) // 8
        cstate = {}

        def gen_batch(c, bi):
            ks = PEK[bi * 8:bi * 8 + 8]
            js = []
            for k in ks:
                j = rot("dg", NDG)
                js.append(j)
                wsc = dww[:, c, k:k + 1]
                if k % 8 == 0:
                    P.op("dve", lambda e, o=dg[:, j, :], s=wsc: e.tensor_scalar(o, ident[:, :], s, None, ALU.mult),
                         reads=[("ident",), ("dww",)], writes=[("dg", j)])
                else:
                    P.op("act", lambda e, o=dg[:, j, :], s=wsc: e.activation(o, ident[:, :], AF.Identity, scale=s),
                         reads=[("ident",), ("dww",)], writes=[("dg", j)])
            cstate[(c, bi)] = (ks, js)

        def pe_batch(c, bi, zi, zk):
            if bi == 0:
                cstate[("bank", c)] = pbank(main)
            cv, ck = cstate[("bank", c)]
            ks, js = cstate.pop((c, bi))

            def fn(e, ks=ks, js=js, cv=cv, zi=zi):
                ins = None
                for k, j in zip(ks, js):
                    ins = e.matmul(cv, dg[:, j, :], zt[:, zi, 2 + k:2 + k + TN], start=(k == DT),
                                   stop=(k == CW - 1), skip_group_check=True)
                return ins
            P.op("pe", fn, reads=[("dg", j) for j in js] + [zk], writes=[ck])

        def conv_tail(c, zi, zk):
            pe_batch(c, 0, zi, zk)
            pe_batch(c, 1, zi, zk)
            gen_batch(c, 2)
            pe_batch(c, 2, zi, zk)
            yk = ("R1", c)
            accs = [(R1[:, c, :], yk), (cacc[:, :], ("cacc",))]
            for k in range(DT):
                ov, ok = accs[k % 2]
                src = zt[:, zi, 2 + k:2 + k + TN]
                wsc = dww[:, c, k:k + 1]
                if k == 0:
                    P.op("dve", lambda e, o=ov, a=src, s=wsc, s2=vecs[:, V_DWB, c:c + 1]:
                         e.tensor_scalar(o, a, s, s2, ALU.mult, ALU.add),
                         reads=[zk, ("dww",), ("vecs",)], writes=[ok])
                elif k == 1:
                    P.op("dve", lambda e, o=ov, a=src, s=wsc: e.tensor_scalar(o, a, s, None, ALU.mult),
                         reads=[zk, ("dww",)], writes=[ok])
                else:
                    P.op("dve", lambda e, o=ov, a=src, s=wsc: e.scalar_tensor_tensor(o, a, s, o, ALU.mult, ALU.add),
                         reads=[zk, ("dww",), ok], writes=[ok])
            cv, ck = cstate.pop(("bank", c))
            P.op("dve", lambda e, o=R1[:, c, :], a=cv: e.tensor_tensor(o, a, o, ALU.add),
                 reads=[ck, yk], writes=[yk])
            P.op("dve", lambda e, o=R1[:, c, :]: e.tensor_tensor(o, o, cacc[:, :], ALU.add),
                 reads=[yk, ("cacc",)], writes=[yk])

        def stats_chunk(c):
            yk = ("R1", c)
            yv = R1[:, c, :]
            i1 = rot("ybf", 2)
            P.op("act", lambda e, o=ybf[:, i1, :], a=yv: e.activation(o, a, AF.Copy),
                 reads=[yk], writes=[("ybf", i1)])
            P.op("pe", lambda e, r=ybf[:, i1, :], c=c: e.matmul(ps[:, 6, :], onesD[:, :], r, start=(c == 0),
                                                               stop=(c == NDC - 1), skip_group_check=True),
                 reads=[("ybf", i1), ("onesD",)], writes=[mb])
            i2 = rot("sq", 2)
            P.op("act", lambda e, o=sq[:, i2, :], a=yv: e.activation(o, a, AF.Square),
                 reads=[yk], writes=[("sq", i2)])
            P.op("pe", lambda e, r=sq[:, i2, :], c=c: e.matmul(ps[:, 7, :], onesD[:, :], r, start=(c == 0),
                                                              stop=(c == NDC - 1), skip_group_check=True),
                 reads=[("sq", i2), ("onesD",)], writes=[qb])

        assert NB == 3
        prev = None
        for c in range(NDC):
            if prev is not None:
                gen_batch(prev[0], 0)
                gen_batch(prev[0], 1)
            cur = glu_chunk(c)
            if prev is not None:
                conv_tail(*prev)
                if prev[0] >= 1:
                    stats_chunk(prev[0] - 1)
            prev = (c,) + cur
        gen_batch(prev[0], 0)
        gen_batch(prev[0], 1)
        conv_tail(*prev)
        stats_chunk(NDC - 2)
        stats_chunk(NDC - 1)
        P.op("act", lambda e: e.activation(mu[:, :], ps[:, 6, :], AF.Copy), reads=[mb], writes=[("mu",)])
        tv, tk = tmp_alloc(main)
        P.op("dve", lambda e, o=tv: e.tensor_tensor(o, mu[:, :], mu[:, :], ALU.mult), reads=[("mu",)], writes=[tk])
        P.op("dve", lambda e, o=tv: e.tensor_tensor(o, ps[:, 7, :], o, ALU.subtract), reads=[qb, tk], writes=[tk])
        P.op("act", lambda e, a=tv: e.activation(rstd[:, :], a, AF.Sqrt, bias=EPS),
             reads=[tk], writes=[("rstd",)])
        P.op("dve", lambda e: e.reciprocal(rstd[:, :], rstd[:, :]), reads=[("rstd",)], writes=[("rstd",)])
        def ln_apply(c):
            t1, k1 = tmp_alloc(main)
            P.op("dve", lambda e, o=t1, a=R1[:, c, :]: e.tensor_tensor(o, a, mu[:, :], ALU.subtract),
                 reads=[("R1", c), ("mu",)], writes=[k1])
            P.op("dve", lambda e, o=t1: e.tensor_tensor(o, o, rstd[:, :], ALU.mult),
                 reads=[k1, ("rstd",)], writes=[k1])
            P.op("act", lambda e, o=R2[:, c, :], a=t1, s=vecs[:, V_CG, c:c + 1], bb=vecs[:, V_CB, c:c + 1]:
                 e.activation(o, a, AF.Silu, bias=bb, scale=s),
                 reads=[k1, ("vecs",)], writes=[("R2", c)])
        for side in (0, 1):
            gcol = (4 + side) * D
            wp = "wa" if side == 0 else "wb"
            src = ya if side == 0 else R2
            srck = (lambda k: ("ya", k)) if side == 0 else (lambda k: ("R2", k))
            for dq in range(4):
                wtg, wkg = wtile("win", 0, NDC, gcol + dq * 512, gcol + (dq + 1) * 512)
                wtp, wkp = wtile(wp, 0, NDC, dq * 512, (dq + 1) * 512)
                sg = []
                if side == 0:
                    for c in range(dq * 4, dq * 4 + 4):
                        ln_apply(c)
                for cl in range(4):
                    gv, gk = pbank(main)
                    mm_group(gv, [(wtg[:, k, cl * 128:(cl + 1) * 128], xn[:, k, :]) for k in range(NDC)],
                             reads=[wkg] + [("xn", k) for k in range(NDC)], writes=[gk])
                    tv, tk = tmp_alloc(main)
                    P.op("act", lambda e, o=tv, a=gv: e.activation(o, a, AF.Sigmoid), reads=[gk], writes=[tk])
                    sg.append((tv, tk))
                for cl in range(4):
                    dm = dq * 4 + cl
                    tv, tk = sg[cl]
                    pv, pk = pbank(main)
                    mm_group(pv, [(wtp[:, k, cl * 128:(cl + 1) * 128], src[:, k, :]) for k in range(NDC)],
                             reads=[wkp] + [srck(k) for k in range(NDC)], writes=[pk])
                    if side == 0:
                        P.op("dve", lambda e, o=R1[:, dm, :], a=tv, b=pv: e.tensor_tensor(o, a, b, ALU.mult),
                             reads=[tk, pk], writes=[("R1", dm)])
                    else:
                        P.op("dve", lambda e, o=tv, b=pv: e.tensor_tensor(o, o, b, ALU.mult),
                             reads=[tk, pk], writes=[tk])
                        P.op("dve", lambda e, o=ya[:, dm, :], a=tv, b=R1[:, dm, :]: e.tensor_tensor(o, a, b, ALU.add),
                             reads=[tk, ("R1", dm)], writes=[("ya", dm)])
        for dq in range(4):
            wt, wk = wtile("wo", 0, NDC, dq * 512, (dq + 1) * 512)
            for cl in range(4):
                dc = dq * 4 + cl
                bv, bk = pbank(main)
                mm_group(bv, [(wt[:, k, cl * 128:(cl + 1) * 128], ya[:, k, :]) for k in range(NDC)],
                         reads=[wk] + [("ya", k) for k in range(NDC)], writes=[bk])
                P.op("dve", lambda e, o=h[:, dc, :], b=bv: e.tensor_tensor(o, o, b, ALU.add),
                     reads=[bk, ("h", dc)], writes=[("h", dc)])

    def ple_and_out(p):
        t0 = p * TN
        rmsnorm_raw(main, V_PLE)
        P.op("pool", lambda e, t0=t0: e.dma_start(out=pTb[:, :, :], in_=pTv[:, :, t0:t0 + TN]),
             writes=[("pTb",)], sem="pld", inc=16)
        def pre_fin(dc):
            P.op("dve", lambda e, o=R1[:, dc, :], a=h[:, dc, :], s=vecs[:, V_FIN, dc:dc + 1]:
                 e.tensor_scalar(o, a, s, None, ALU.mult),
                 reads=[("h", dc), ("vecs",)], writes=[("R1", dc)])
        for dq in range(4):
            wt, wk = wtile("wpg", 0, NDC, dq * 512, (dq + 1) * 512)
            wpp_t, wpp_k = wtile("wpp", 0, 2, dq * 512, (dq + 1) * 512)
            sg = []
            for cl in range(4):
                gv, gk = pbank(main)
                mm_group(gv, [(wt[:, k, cl * 128:(cl + 1) * 128], xn[:, k, :]) for k in range(NDC)],
                         reads=[wk] + [("xn", k) for k in range(NDC)], writes=[gk])
                tv, tk = tmp_alloc(main)
                P.op("dve", lambda e, o=tv, a=gv: e.tensor_tensor(o, a, rstd[:, :], ALU.mult),
                     reads=[gk, ("rstd",)], writes=[tk])
                P.op("act", lambda e, o=tv: e.activation(o, o, AF.Sigmoid), reads=[tk], writes=[tk])
                sg.append((tv, tk))
                if dq >= 1 and cl == 1:
                    for dc in range(4 * dq - 4, 4 * dq):
                        rms_stat_chunk(main, main.h, main.hk, dc, pre_fin)
            for cl in range(4):
                dc = dq * 4 + cl
                tv, tk = sg[cl]
                pv, pk = pbank(main)
                mm_group(pv, [(wpp_t[:, k, cl * 128:(cl + 1) * 128], pTb[:, k, :]) for k in range(2)],
                         reads=[wpp_k, ("pTb",)], writes=[pk])
                P.op("dve", lambda e, o=tv, b=pv: e.tensor_tensor(o, o, b, ALU.mult),
                     reads=[tk, pk], writes=[tk])
                P.op("dve", lambda e, o=h[:, dc, :], a=tv: e.tensor_tensor(o, o, a, ALU.add),
                     reads=[tk, ("h", dc)], writes=[("h", dc)])
        for dc in range(12, 16):
            rms_stat_chunk(main, main.h, main.hk, dc, pre_fin)
        rms_finish(main, dest=mu[:, :], destk=("mu",))
        if p == 0:
            load_x(1)
        return lambda: emit_out(t0)

    def emit_out(t0):
        for dc in range(NDC):
            oi = rot("o", 4)
            ov, ok = tmp[:, oi, :], ("tmp", oi)
            P.op("dve", lambda e, o=ov, a=R1[:, dc, :]: e.tensor_tensor(o, a, mu[:, :], ALU.mult),
                 reads=[("R1", dc), ("mu",)], writes=[ok])
            P.op("sp", lambda e, o=oTv[:, dc, t0:t0 + TN], a=ov: e.dma_start(out=o, in_=a),
                 reads=[ok], sem="o%d" % oi, inc=16)

    cdma = []

    def cload(eng, dst, src, key):
        P.op(eng, lambda e: e.dma_start(out=dst, in_=src), writes=[key], sem="cst", inc=16)
        cdma.append(key)
    cload("sp", vecs[:, :, :], vecs_d.rearrange("p (v k) -> p v k", k=NDC), ("vecs",))
    cload("sp", dww[:, :, :], dww_d.rearrange("p (c k) -> p c k", k=CW), ("dww",))
    cload("sp", bsB, bsB_d.rearrange("p (g i) -> p g i", i=128), ("bsB",))
    cload("sp", hmask[:, :], hmask_d, ("hmask",))
    cload("sp", hh[:, :, :], xhv, ("hhall",))
    P.op("pool", lambda e: e.dma_start(out=wmT[:, :, :], in_=wsT_d.rearrange("p (g i) -> p g i", i=128)),
         writes=[("wmT",)], sem="cst2", inc=16)
    for k in cdma:
        P.lastw[k] = ("cst", P.cnt["cst"])
    for dc in range(NDC):
        P.lastw[("hh", dc)] = ("cst", P.cnt["cst"])
    P.op("dve", lambda e: e.memset(ones1[:, :], 1.0), writes=[("ones1",)])
    P.op("dve", lambda e: e.memset(onesD[:, :], 1.0 / D), writes=[("onesD",)])
    P.op("dve", lambda e: e.memset(ident[:, :], 1.0), writes=[("ident",)])
    P.op("pool", lambda e: e.affine_select(ident[:, :], ident[:, :], [[-1, 128]], ALU.is_equal, 0.0,
                                           base=0, channel_multiplier=1),
         reads=[("ident",)], writes=[("ident",)])
    P.op("dve", lambda e: e.memset(wmT[64:128, :, 0:64], 0.0), reads=[("wmT",)], writes=[("wmT",)])
    for half in range(2):
        b = bank()
        for gl in range(4):
            g = half * 4 + gl
            mm_group(ps[:, b, gl * 128:(gl + 1) * 128], [(ones1[:, :], wmT[:, g, :])],
                     reads=[("ones1",), ("wmT",)], writes=[("ps", b)], force_start=(gl == 0))
        for gl in range(4):
            g = half * 4 + gl
            for c in (2 * g, 2 * g + 1):
                P.op("dve", lambda e, o=Cc[:, c, :], a=ps[:, b, gl * 128:(gl + 1) * 128], s=vecs[:, V_SB, c:c + 1],
                     bb=bsB[:, g, :]: e.scalar_tensor_tensor(o, a, s, bb, ALU.mult, ALU.add),
                     reads=[("ps", b), ("vecs",), ("bsB",)], writes=[("Cc",)])
    for i in range(2):
        P.lastw[("R1", i)] = P.lastw[("Cc",)]

    def load_x(p):
        t0 = p * TN
        for q in range(4):
            P.op("sp", lambda e, t0=t0, q=q: e.dma_start(out=h[:, 4 * q:4 * q + 4, :], in_=xTv[:, 4 * q:4 * q + 4, t0:t0 + TN]),
                 writes=[("h", dc) for dc in range(4 * q, 4 * q + 4)], sem="xld%d" % q, inc=16)

    load_x(0)
    pending_out = None
    for p in range(2):
        ffn([main, halo] if p == 0 else [main], V_F1, "f1g", "f1u", "f1d", after_norm=pending_out)
        mix(p)
        ffn([main], V_F2, "f2g", "f2u", "f2d")
        pending_out = ple_and_out(p)
    pending_out()
    P.wait_all("sp", ["o0", "o1", "o2", "o3"])

    sem_names = sorted(P.cnt.keys())
    sems = {s: st.enter_context(nc.semaphore(s)) for s in sem_names}

    def run(name, e):
        for waits, fn, s, inc in P.ops[name]:
            for ws, wv in waits:
                e.wait_ge(sems[ws], wv)
            if fn is not None:
                fn(e).then_inc(sems[s], inc)

    with nc.Block() as block:
        @block.tensor
        def _(e):
            run("pe", e)

        @block.scalar
        def _(e):
            run("act", e)

        @block.vector
        def _(e):
            run("dve", e)

        @block.gpsimd
        def _(e):
            run("pool", e)

        @block.sync
        def _(e):
            run("sp", e)
    st.close()
    return nc


_NC = None


def _prep_inputs(x, p, ffn1_norm, ffn1_w_gate, ffn1_w_up, ffn1_w_down, mix_norm, w_in,
                 sgu_ln_g, sgu_ln_b, sgu_w, sgu_b, w_a_proj, dw_w, dw_b, conv_ln_g,
                 conv_ln_b, w_b_proj, w_out, ffn2_norm, ffn2_w_gate, ffn2_w_up,
                 ffn2_w_down, ple_norm, w_ple_gate, w_ple_proj, final_norm):
    f = lambda a: np.ascontiguousarray(np.asarray(a, dtype=np.float32))
    x2 = f(x)[0]
    p2 = f(p)[0, 0]
    vl = [ffn1_norm[0], mix_norm[0], sgu_ln_g[0], sgu_ln_b[0], dw_b[0], conv_ln_g[0], conv_ln_b[0],
          ffn2_norm[0], ple_norm[0], final_norm]
    vecs = np.stack([f(v).reshape(NDC, 128).T for v in vl], axis=1)
    vecs = f(vecs.reshape(128, NV * NDC))
    dww = f(f(dw_w)[0].T.reshape(NDC, 128, CW).transpose(1, 0, 2).reshape(128, NDC * CW))
    wsT = f(f(sgu_w)[0].transpose(2, 0, 1).reshape(128, 8 * 128))
    bsB = f(np.broadcast_to(f(sgu_b)[0].reshape(1, 8 * 128), (128, 8 * 128)))
    shared = {
        "vecs": vecs, "dww": dww, "wsT": wsT, "bsB": bsB,
        "f1g": f(ffn1_w_gate)[0], "f1u": f(ffn1_w_up)[0], "f1d": f(ffn1_w_down)[0],
        "win": f(w_in)[0], "wa": f(w_a_proj)[0], "wb": f(w_b_proj)[0], "wo": f(w_out)[0],
        "f2g": f(ffn2_w_gate)[0], "f2u": f(ffn2_w_up)[0], "f2d": f(ffn2_w_down)[0],
        "wpg": f(w_ple_gate)[0], "wpp": f(w_ple_proj)[0],
    }
    in_maps = []
    for c in range(NCORES):
        s0 = c * T
        m = dict(shared)
        m["xT"] = f(x2[s0:s0 + T].T)
        m["pT"] = f(p2[s0:s0 + T].T)
        if c == 0:
            m["xh"] = np.zeros((D, HN), np.float32)
            m["hmask"] = np.zeros((128, 1), np.float32)
        else:
            m["xh"] = f(x2[s0 - HN:s0].T)
            m["hmask"] = np.ones((128, 1), np.float32)
        in_maps.append(m)
    return in_maps


def kernel(**inputs):
    global _NC
    in_maps = _prep_inputs(**inputs)
    if _NC is None:
        _NC = build_nc()
    res = run_bass_kernel_spmd(_NC, in_maps, core_ids=list(range(NCORES)))
    out = np.concatenate([np.asarray(r["oT"], dtype=np.float32).T for r in res.results], axis=0)
    return out.reshape(1, NCORES * T, D)
```

```python
import numpy as np
from contextlib import ExitStack
import concourse.bass as bass
import concourse.mybir as mybir
from concourse.bass_utils import run_bass_kernel_spmd

F32 = mybir.dt.float32
BF16 = mybir.dt.bfloat16
ALU = mybir.AluOpType
AF = mybir.ActivationFunctionType

D = 2048
FF = 5632
T = 1024
TN = 512
HN = 32
NDC = 16
NFC = 44
NCORES = 8
EPS = 1e-6
CW = 31
NS = 3
V_F1, V_MIX, V_SG, V_SB, V_DWB, V_CG, V_CB, V_F2, V_PLE, V_FIN = range(10)
NV = 10


class Prog:
    ENGS = ("pe", "act", "dve", "pool", "sp")

    def __init__(self):
        self.ops = {e: [] for e in self.ENGS}
        self.cnt = {}
        self.waited = {e: {} for e in self.ENGS}
        self.lastw = {}
        self.readers = {}

    def op(self, eng, fn, reads=(), writes=(), sem=None, inc=1):
        need = {}

        def add(s, v):
            if v > need.get(s, 0):
                need[s] = v
        for k in reads:
            w = self.lastw.get(k)
            if w:
                add(*w)
        for k in writes:
            w = self.lastw.get(k)
            if w:
                add(*w)
            for s, v in self.readers.get(k, {}).items():
                add(s, v)
        waits = []
        for s, v in need.items():
            if eng == "pe" and s == "pe":
                continue
            if self.waited[eng].get(s, 0) >= v:
                continue
            self.waited[eng][s] = v
            waits.append((s, v))
        s = sem or eng
        self.cnt[s] = self.cnt.get(s, 0) + inc
        v = self.cnt[s]
        self.ops[eng].append((waits, fn, s, inc))
        for k in reads:
            self.readers.setdefault(k, {})[s] = v
        for k in writes:
            self.lastw[k] = (s, v)
            self.readers[k] = {}
        return v

    def wait_all(self, eng, sems):
        waits = [(s, self.cnt[s]) for s in sems if self.cnt.get(s, 0) > 0]
        self.ops[eng].append((waits, None, None, 0))


def build_nc():
    nc = bass.Bass("TRN2", target_bir_lowering=False)
    P = Prog()

    def din(name, shape):
        return nc.dram_tensor(name, shape, F32, kind="ExternalInput").ap()

    xT = din("xT", [D, T])
    xh = din("xh", [D, HN])
    pT = din("pT", [256, T])
    vecs_d = din("vecs", [128, NV * NDC])
    dww_d = din("dww", [128, NDC * CW])
    wsT_d = din("wsT", [128, 8 * 128])
    bsB_d = din("bsB", [128, 8 * 128])
    hmask_d = din("hmask", [128, 1])
    Wd_ = {}
    for nm, shp in (("f1g", [D, FF]), ("f1u", [D, FF]), ("f1d", [FF, D]), ("win", [D, 6 * D]),
                    ("wa", [D, D]), ("wb", [D, D]), ("wo", [D, D]),
                    ("f2g", [D, FF]), ("f2u", [D, FF]), ("f2d", [FF, D]),
                    ("wpg", [D, D]), ("wpp", [256, D])):
        Wd_[nm] = din(nm, shp).rearrange("(k p) n -> p k n", p=128)
    oT = nc.dram_tensor("oT", [D, T], F32, kind="ExternalOutput").ap()
    oTv = oT.rearrange("(k p) n -> p k n", p=128)
    xTv = xT.rearrange("(k p) n -> p k n", p=128)
    xhv = xh.rearrange("(k p) n -> p k n", p=128)
    pTv = pT.rearrange("(k p) n -> p k n", p=128)

    st = ExitStack()
    sb = lambda name, shape, dt: st.enter_context(nc.sbuf_tensor(name, shape, dt))
    h = sb("h", [128, NDC, TN], F32)
    hh = sb("hh", [128, NDC, HN], F32)
    ztail = sb("ztail", [128, NDC, HN], BF16)
    ring = sb("ring", [128, NS, 8192], BF16)
    AW = 80 * 256 + 704
    arena = sb("arena", [128, AW], F32)
    tmp = sb("tmp", [128, 4, TN], F32)
    tmph = sb("tmph", [128, 4, HN], F32)
    sq = sb("sq", [128, 2, TN], BF16)
    ybf = sb("ybf", [128, 2, TN], BF16)
    zt = sb("zt", [128, 2, HN + TN], BF16)
    ident = sb("ident", [128, 128], BF16)
    cacc = sb("cacc", [128, TN], F32)
    NDG = 16
    dg = sb("dg", [128, NDG, 128], BF16)
    rstd = sb("rstd", [128, TN], F32)
    mu = sb("mu", [128, TN], F32)
    rstdh = sb("rstdh", [128, HN], F32)
    xnh = sb("xnh", [128, NDC, HN], BF16)
    ones1 = sb("ones1", [128, 128], BF16)
    onesD = sb("onesD", [128, 128], BF16)
    vecs = sb("vecs_sb", [128, NV, NDC], F32)
    dww = sb("dww_sb", [128, NDC, CW], F32)
    wmT = sb("wmT", [128, 8, 128], BF16)
    Cc = sb("Cc", [128, NDC, 128], F32)
    pTb = sb("pTb", [128, 2, TN], BF16)
    hmask = sb("hmask_sb", [128, 1], F32)
    vstats = sb("vstats", [128, 4, 4, 6], F32)
    vmv = sb("vmv", [128, 4, 2], F32)
    vrs = sb("vrs", [128, 4], F32)
    vnm = sb("vnm", [128, 4], F32)
    ps = st.enter_context(nc.psum_tensor("ps", [128, 8, 512], F32))

    def a_bf(lo, hi, n):
        return arena[:, lo:hi].bitcast(BF16).rearrange("p (k n) -> p k n", n=n)

    def a_f32(lo, hi, n):
        return arena[:, lo:hi].rearrange("p (k n) -> p k n", n=n)
    K = 256
    xn = a_bf(0, 16 * K, TN)
    gbuf = a_bf(16 * K, 60 * K, TN)
    R1 = a_f32(16 * K, 48 * K, TN)
    vtm = arena[:, 16 * K:48 * K].rearrange("p (t c) -> p t c", c=D)
    R2 = a_bf(48 * K, 64 * K, TN)
    vhat = arena[:, 48 * K:64 * K].bitcast(BF16).rearrange("p (t c) -> p t c", c=D)
    ya = a_bf(64 * K, 80 * K, TN)
    gh = a_bf(80 * K, 80 * K + 704, HN)
    bsB = arena[:, 16 * K:20 * K].rearrange("p (g i) -> p g i", i=128)

    def gkey(f):
        return ("R1", f // 2) if f < 32 else ("R2", f - 32)

    cnt = {"bank": 0, "tmp": 0, "tmph": 0, "sq": 0, "w": 0, "ybf": 0, "zt": 0, "o": 0, "dg": 0}

    def bank():
        b = cnt["bank"] % 6
        cnt["bank"] += 1
        return b

    def rot(name, n):
        i = cnt[name] % n
        cnt[name] += 1
        return i

    def wtile(wname, k0, k1, c0, c1):
        s = rot("w", NS)
        nk, ncol = k1 - k0, c1 - c0
        view = ring[:, s, 0:nk * ncol].rearrange("p (k n) -> p k n", n=ncol)
        src = Wd_[wname][:, k0:k1, c0:c1]
        P.op("pool", lambda e, view=view, src=src: e.dma_start(out=view, in_=src),
             writes=[("w", s)], sem="w%d" % s, inc=16)
        return view, ("w", s)

    def wtile_pair(wname, ca, cb, ncol):
        s = rot("w", NS)
        view = ring[:, s, 0:NDC * 2 * ncol].rearrange("p (k n) -> p k n", n=2 * ncol)
        srca = Wd_[wname][:, 0:NDC, ca:ca + ncol]
        srcb = Wd_[wname][:, 0:NDC, cb:cb + ncol]
        P.op("pool", lambda e: e.dma_start(out=view[:, :, 0:ncol], in_=srca),
             writes=[("w", s)], sem="w%d" % s, inc=16)
        P.op("pool", lambda e: e.dma_start(out=view[:, :, ncol:2 * ncol], in_=srcb),
             sem="w%d" % s, inc=16)
        P.lastw[("w", s)] = ("w%d" % s, P.cnt["w%d" % s])
        return view, ("w", s)

    def mm_group(out_ap, pairs, reads, writes, first=True, last=True, force_start=None):
        def fn(e):
            ins = None
            n = len(pairs)
            for i, (l, r) in enumerate(pairs):
                stt = (first and i == 0) if force_start is None else (force_start and i == 0)
                ins = e.matmul(out_ap, l, r, start=stt, stop=(last and i == n - 1),
                               skip_group_check=True)
            return ins
        P.op("pe", fn, reads=reads, writes=writes)

    def mm_group_split(out_ap, pairs, pair_keys, common_reads, writes):
        n = len(pairs)
        for i, ((l, r), pk) in enumerate(zip(pairs, pair_keys)):
            P.op("pe", lambda e, l=l, r=r, i=i: e.matmul(out_ap, l, r, start=(i == 0), stop=(i == n - 1),
                                                         skip_group_check=True),
                 reads=list(common_reads) + [pk], writes=writes)

    class Tl:
        pass

    main = Tl()
    main.n = TN
    main.h = lambda dc: h[:, dc, :]
    main.hk = lambda dc: ("h", dc)
    main.xn = lambda dc: xn[:, dc, :]
    main.xnk = lambda dc: ("xn", dc)
    main.g = lambda f: gbuf[:, f, :]
    main.gk = gkey
    main.rstd = rstd[:, :]
    main.rstdk = ("rstd",)
    main.sbank = 6
    main.halo = False
    halo = Tl()
    halo.n = HN
    halo.h = lambda dc: hh[:, dc, :]
    halo.hk = lambda dc: ("hh", dc)
    halo.xn = lambda dc: xnh[:, dc, :]
    halo.xnk = lambda dc: ("xnh", dc)
    halo.g = lambda f: gh[:, f, :]
    halo.gk = lambda f: ("gh", f)
    halo.rstd = rstdh[:, :]
    halo.rstdk = ("rstdh",)
    halo.sbank = 7
    halo.halo = True

    def tmp_alloc(tl):
        if tl.halo:
            i = rot("tmph", 4)
            return tmph[:, i, :], ("tmph", i)
        i = rot("tmp", 4)
        return tmp[:, i, :], ("tmp", i)

    def pbank(tl):
        b = bank()
        return ps[:, b, 0:tl.n], ("ps", b)

    def rms_stat_chunk(tl, src, srck, dc, pre=None):
        sbk = ("ps", tl.sbank)
        sbv = ps[:, tl.sbank, 0:tl.n]
        if pre is not None:
            pre(dc)
        i = rot("sq", 2)
        sqv = sq[:, i, 0:tl.n]
        P.op("act", lambda e, o=sqv, a=src(dc): e.activation(o, a, AF.Square),
             reads=[srck(dc)], writes=[("sq", i)])
        P.op("pe", lambda e, o=sbv, r=sqv, dc=dc: e.matmul(o, onesD[:, :], r, start=(dc == 0),
                                                          stop=(dc == NDC - 1), skip_group_check=True),
             reads=[("sq", i), ("onesD",)], writes=[sbk])

    def rms_finish(tl, dest=None, destk=None):
        sbk = ("ps", tl.sbank)
        sbv = ps[:, tl.sbank, 0:tl.n]
        dv = tl.rstd if dest is None else dest
        dk = tl.rstdk if destk is None else destk
        P.op("act", lambda e, o=dv, a=sbv: e.activation(o, a, AF.Sqrt, bias=EPS),
             reads=[sbk], writes=[dk])
        P.op("dve", lambda e, o=dv: e.reciprocal(o, o), reads=[dk], writes=[dk])

    def rms_stats(tl, src, srck, pre=None):
        for dc in range(NDC):
            rms_stat_chunk(tl, src, srck, dc, pre)
        rms_finish(tl)

    def rmsnorm(tl, vi):
        rms_stats(tl, tl.h, tl.hk)
        for dc in range(NDC):
            P.op("dve", lambda e, o=tl.xn(dc), a=tl.h(dc), s=vecs[:, vi, dc:dc + 1], r=tl.rstd:
                 e.scalar_tensor_tensor(o, a, s, r, ALU.mult, ALU.mult),
                 reads=[tl.hk(dc), tl.rstdk, ("vecs",)], writes=[tl.xnk(dc)])

    def rmsnorm_raw(tl, vi):
        def pre(dc):
            P.op("dve", lambda e, o=tl.xn(dc), a=tl.h(dc), s=vecs[:, vi, dc:dc + 1]:
                 e.tensor_scalar(o, a, s, None, ALU.mult),
                 reads=[tl.hk(dc), ("vecs",)], writes=[tl.xnk(dc)])
        rms_stats(tl, tl.h, tl.hk, pre=pre)

    def ffn(tiles, vi, wg, wu, wd, after_norm=None):
        for tl in tiles:
            rmsnorm_raw(tl, vi)
        if after_norm is not None:
            after_norm()
        for fq in range(NFC // 4):
            c0 = fq * 512
            sgs = {}
            wt, wk = wtile(wg, 0, NDC, c0, c0 + 512)
            for fl in range(4):
                for ti, tl in enumerate(tiles):
                    bv, bk = pbank(tl)
                    mm_group(bv, [(wt[:, k, fl * 128:(fl + 1) * 128], tl.xn(k)) for k in range(NDC)],
                             reads=[wk] + [tl.xnk(k) for k in range(NDC)], writes=[bk])
                    tv, tk = tmp_alloc(tl)
                    P.op("dve", lambda e, o=tv, a=bv, r=tl.rstd: e.tensor_tensor(o, a, r, ALU.mult),
                         reads=[bk, tl.rstdk], writes=[tk])
                    P.op("act", lambda e, o=tv: e.activation(o, o, AF.Silu), reads=[tk], writes=[tk])
                    P.op("dve", lambda e, o=tv, r=tl.rstd: e.tensor_tensor(o, o, r, ALU.mult),
                         reads=[tk, tl.rstdk], writes=[tk])
                    sgs[(fl, ti)] = (tv, tk)
            wt, wk = wtile(wu, 0, NDC, c0, c0 + 512)
            for fl in range(4):
                f = fq * 4 + fl
                for ti, tl in enumerate(tiles):
                    bv, bk = pbank(tl)
                    mm_group(bv, [(wt[:, k, fl * 128:(fl + 1) * 128], tl.xn(k)) for k in range(NDC)],
                             reads=[wk] + [tl.xnk(k) for k in range(NDC)], writes=[bk])
                    tv, tk = sgs[(fl, ti)]
                    P.op("dve", lambda e, o=tl.g(f), a=tv, b=bv: e.tensor_tensor(o, a, b, ALU.mult),
                         reads=[tk, bk], writes=[tl.gk(f)])
        kgs = [(0, 16), (16, 32), (32, 44)]
        for cq in range(4):
            c0 = cq * 512
            banks = {}
            hb = bank() if len(tiles) > 1 else None
            for gi, (k0, k1) in enumerate(kgs):
                wt, wk = wtile(wd, k0, k1, c0, c0 + 512)
                for cl in range(4):
                    for ti, tl in enumerate(tiles):
                        if ti == 0:
                            if gi == 0:
                                banks[cl] = pbank(tl)
                            bv, bk = banks[cl]
                            fs = None
                        else:
                            bv, bk = ps[:, hb, cl * HN:(cl + 1) * HN], ("ps", hb)
                            fs = (gi == 0 and cl == 0)
                        mm_group(bv, [(wt[:, k - k0, cl * 128:(cl + 1) * 128], tl.g(k)) for k in range(k0, k1)],
                                 reads=[wk] + [tl.gk(k) for k in range(k0, k1)], writes=[bk],
                                 first=(gi == 0), last=(gi == 2), force_start=fs)
            for cl in range(4):
                dc = cq * 4 + cl
                for ti, tl in enumerate(tiles):
                    if ti == 0:
                        bv, bk = banks[cl]
                    else:
                        bv, bk = ps[:, hb, cl * HN:(cl + 1) * HN], ("ps", hb)
                    P.op("dve", lambda e, o=tl.h(dc), b=bv: e.scalar_tensor_tensor(o, b, 0.5, o, ALU.mult, ALU.add),
                         reads=[bk, tl.hk(dc)], writes=[tl.hk(dc)])

    def mix(p):
        tiles = [main, halo] if p == 0 else [main]
        for tl in tiles:
            rmsnorm(tl, V_MIX)
        for cb in range(4):
            wt, wk = wtile("win", 0, NDC, D + cb * 512, D + (cb + 1) * 512)
            for tb in range(4):
                b = bank()
                bv, bk = ps[:, b, :], ("ps", b)
                if cb == 0 and tb == 0:
                    mm_group_split(bv, [(xn[:, k, tb * 128:(tb + 1) * 128], wt[:, k, :]) for k in range(NDC)],
                                   [("xn", k) for k in range(NDC)], [wk], [bk])
                else:
                    mm_group(bv, [(xn[:, k, tb * 128:(tb + 1) * 128], wt[:, k, :]) for k in range(NDC)],
                             reads=[wk] + [("xn", k) for k in range(NDC)], writes=[bk])
                gran = ("R1", tb * 4 + cb)
                P.op("act", lambda e, o=vtm[:, tb, cb * 512:(cb + 1) * 512], a=bv: e.activation(o, a, AF.Copy),
                     reads=[bk], writes=[gran])
                P.op("dve", lambda e, o=vstats[:, tb, cb, :], a=vtm[:, tb, cb * 512:(cb + 1) * 512]: e.bn_stats(o, a),
                     reads=[gran], writes=[("vstats", tb, cb)])
        for tb in range(4):
            P.op("dve", lambda e, o=vmv[:, tb, :], a=vstats[:, tb, :, :].rearrange("p a b -> p (a b)"): e.bn_aggr(o, a),
                 reads=[("vstats", tb, cb) for cb in range(4)], writes=[("vmv", tb)])
            P.op("act", lambda e, o=vrs[:, tb:tb + 1], a=vmv[:, tb, 1:2]: e.activation(o, a, AF.Sqrt, bias=EPS),
                 reads=[("vmv", tb)], writes=[("vrs", tb)])
            P.op("dve", lambda e, o=vrs[:, tb:tb + 1]: e.reciprocal(o, o), reads=[("vrs", tb)], writes=[("vrs", tb)])
            P.op("dve", lambda e, o=vnm[:, tb:tb + 1], a=vmv[:, tb, 0:1], s=vrs[:, tb:tb + 1]:
                 e.tensor_scalar(o, a, s, -1.0, ALU.mult, ALU.mult),
                 reads=[("vmv", tb), ("vrs", tb)], writes=[("vnm", tb)])
            P.op("act", lambda e, o=vhat[:, tb, :], a=vtm[:, tb, :], s=vrs[:, tb:tb + 1], bb=vnm[:, tb:tb + 1]:
                 e.activation(o, a, AF.Identity, bias=bb, scale=s),
                 reads=[("R1", tb * 4 + i) for i in range(4)] + [("vrs", tb), ("vnm", tb)],
                 writes=[("R2", tb * 4 + i) for i in range(4)])
        for uq in range(4):
            wt, wk = wtile("win", 0, NDC, uq * 512, (uq + 1) * 512)
            for cl in range(4):
                c = uq * 4 + cl
                g = c // 2
                b = bank()
                sv, sk = ps[:, b, :], ("ps", b)
                for tb in range(4):
                    mm_group(ps[:, b, tb * 128:(tb + 1) * 128],
                             [(vhat[:, tb, c * 128:(c + 1) * 128], wmT[:, g, :])],
                             reads=[("R2", tb * 4 + c // 4), ("wmT",)], writes=[sk],
                             force_start=(tb == 0))
                uv, uk = pbank(main)
                mm_group(uv, [(wt[:, k, cl * 128:(cl + 1) * 128], xn[:, k, :]) for k in range(NDC)],
                         reads=[wk] + [("xn", k) for k in range(NDC)], writes=[uk])
                tv, tk = tmp_alloc(main)
                P.op("dve", lambda e, o=tv.rearrange("p (t i) -> p t i", i=128),
                     a=sv.rearrange("p (t i) -> p t i", i=128), s=vecs[:, V_SG, c:c + 1],
                     cc=Cc[:, c:c + 1, :].to_broadcast([128, 4, 128]):
                     e.scalar_tensor_tensor(o, a, s, cc, ALU.mult, ALU.add),
                     reads=[sk, ("Cc",), ("vecs",)], writes=[tk])
                P.op("dve", lambda e, o=ya[:, c, :], a=tv, bb=uv: e.tensor_tensor(o, a, bb, ALU.mult),
                     reads=[tk, uk], writes=[("ya", c)])
        mb, qb = ("ps", 6), ("ps", 7)
        wts = {}

        def glu_chunk(c):
            hq, cl = divmod(c, 2)
            if cl == 0:
                wts["ab"] = wtile_pair("win", 2 * D + hq * 256, 3 * D + hq * 256, 256)
            wab, wkab = wts["ab"]
            wta, wka = wab[:, :, 0:256], wkab
            wtb, wkb = wab[:, :, 256:512], wkab
            zi = rot("zt", 2)
            zk = ("zt", zi)
            res = {}
            abanks = {}
            for ti, tl in enumerate(tiles):
                av, ak = pbank(tl)
                mm_group(av, [(wta[:, k, cl * 128:(cl + 1) * 128], tl.xn(k)) for k in range(NDC)],
                         reads=[wka] + [tl.xnk(k) for k in range(NDC)], writes=[ak])
                abanks[ti] = (av, ak)
            for ti, tl in enumerate(tiles):
                av, ak = abanks[ti]
                bv, bk = pbank(tl)
                mm_group(bv, [(wtb[:, k, cl * 128:(cl + 1) * 128], tl.xn(k)) for k in range(NDC)],
                         reads=[wkb] + [tl.xnk(k) for k in range(NDC)], writes=[bk])
                tv, tk = tmp_alloc(tl)
                P.op("act", lambda e, o=tv, a=bv: e.activation(o, a, AF.Sigmoid), reads=[bk], writes=[tk])
                res[ti] = (av, ak, tv, tk)
            av, ak, tv, tk = res[0]
            if p == 0:
                hv, hk_, htv, htk = res[1]
                P.op("dve", lambda e, o=zt[:, zi, 0:HN], a=hv, s=hmask[:, 0:1], t=htv:
                     e.scalar_tensor_tensor(o, a, s, t, ALU.mult, ALU.mult),
                     reads=[hk_, htk, ("hmask",)], writes=[zk])
            else:
                P.op("act", lambda e, o=zt[:, zi, 0:HN], a=ztail[:, c, :]: e.activation(o, a, AF.Copy),
                     reads=[("ztail", c)], writes=[zk])
            P.op("dve", lambda e, o=zt[:, zi, HN:HN + TN], a=av, t=tv: e.tensor_tensor(o, a, t, ALU.mult),
                 reads=[ak, tk], writes=[zk])
            if p == 0:
                P.op("act", lambda e, o=ztail[:, c, :], a=zt[:, zi, TN:TN + HN]: e.activation(o, a, AF.Copy),
                     reads=[zk], writes=[("ztail", c)])
            return zi, zk

        DT = 12
        PEK = list(range(DT, CW))
        NB = (len(PEK) + 7) // 8
        cstate = {}

        def gen_batch(c, bi):
            ks = PEK[bi * 8:bi * 8 + 8]
            js = []
            for k in ks:
                j = rot("dg", NDG)
                js.append(j)
                wsc = dww[:, c, k:k + 1]
                if k % 8 == 0:
                    P.op("dve", lambda e, o=dg[:, j, :], s=wsc: e.tensor_scalar(o, ident[:, :], s, None, ALU.mult),
                         reads=[("ident",), ("dww",)], writes=[("dg", j)])
                else:
                    P.op("act", lambda e, o=dg[:, j, :], s=wsc: e.activation(o, ident[:, :], AF.Identity, scale=s),
                         reads=[("ident",), ("dww",)], writes=[("dg", j)])
            cstate[(c, bi)] = (ks, js)

        def pe_batch(c, bi, zi, zk):
            if bi == 0:
                cstate[("bank", c)] = pbank(main)
            cv, ck = cstate[("bank", c)]
            ks, js = cstate.pop((c, bi))

            def fn(e, ks=ks, js=js, cv=cv, zi=zi):
                ins = None
                for k, j in zip(ks, js):
                    ins = e.matmul(cv, dg[:, j, :], zt[:, zi, 2 + k:2 + k + TN], start=(k == DT),
                                   stop=(k == CW - 1), skip_group_check=True)
                return ins
            P.op("pe", fn, reads=[("dg", j) for j in js] + [zk], writes=[ck])

        def conv_tail(c, zi, zk):
            pe_batch(c, 0, zi, zk)
            pe_batch(c, 1, zi, zk)
            gen_batch(c, 2)
            pe_batch(c, 2, zi, zk)
            yk = ("R1", c)
            accs = [(R1[:, c, :], yk), (cacc[:, :], ("cacc",))]
            for k in range(DT):
                ov, ok = accs[k % 2]
                src = zt[:, zi, 2 + k:2 + k + TN]
                wsc = dww[:, c, k:k + 1]
                if k == 0:
                    P.op("dve", lambda e, o=ov, a=src, s=wsc, s2=vecs[:, V_DWB, c:c + 1]:
                         e.tensor_scalar(o, a, s, s2, ALU.mult, ALU.add),
                         reads=[zk, ("dww",), ("vecs",)], writes=[ok])
                elif k == 1:
                    P.op("dve", lambda e, o=ov, a=src, s=wsc: e.tensor_scalar(o, a, s, None, ALU.mult),
                         reads=[zk, ("dww",)], writes=[ok])
                else:
                    P.op("dve", lambda e, o=ov, a=src, s=wsc: e.scalar_tensor_tensor(o, a, s, o, ALU.mult, ALU.add),
                         reads=[zk, ("dww",), ok], writes=[ok])
            cv, ck = cstate.pop(("bank", c))
            P.op("dve", lambda e, o=R1[:, c, :], a=cv: e.tensor_tensor(o, a, o, ALU.add),
                 reads=[ck, yk], writes=[yk])
            P.op("dve", lambda e, o=R1[:, c, :]: e.tensor_tensor(o, o, cacc[:, :], ALU.add),
                 reads=[yk, ("cacc",)], writes=[yk])

        def stats_chunk(c):
            yk = ("R1", c)
            yv = R1[:, c, :]
            i1 = rot("ybf", 2)
            P.op("act", lambda e, o=ybf[:, i1, :], a=yv: e.activation(o, a, AF.Copy),
                 reads=[yk], writes=[("ybf", i1)])
            P.op("pe", lambda e, r=ybf[:, i1, :], c=c: e.matmul(ps[:, 6, :], onesD[:, :], r, start=(c == 0),
                                                               stop=(c == NDC - 1), skip_group_check=True),
                 reads=[("ybf", i1), ("onesD",)], writes=[mb])
            i2 = rot("sq", 2)
            P.op("act", lambda e, o=sq[:, i2, :], a=yv: e.activation(o, a, AF.Square),
                 reads=[yk], writes=[("sq", i2)])
            P.op("pe", lambda e, r=sq[:, i2, :], c=c: e.matmul(ps[:, 7, :], onesD[:, :], r, start=(c == 0),
                                                              stop=(c == NDC - 1), skip_group_check=True),
                 reads=[("sq", i2), ("onesD",)], writes=[qb])

        assert NB == 3
        prev = None
        for c in range(NDC):
            if prev is not None:
                gen_batch(prev[0], 0)
                gen_batch(prev[0], 1)
            cur = glu_chunk(c)
            if prev is not None:
                conv_tail(*prev)
                if prev[0] >= 1:
                    stats_chunk(prev[0] - 1)
            prev = (c,) + cur
        gen_batch(prev[0], 0)
        gen_batch(prev[0], 1)
        conv_tail(*prev)
        stats_chunk(NDC - 2)
        stats_chunk(NDC - 1)
        P.op("act", lambda e: e.activation(mu[:, :], ps[:, 6, :], AF.Copy), reads=[mb], writes=[("mu",)])
        tv, tk = tmp_alloc(main)
        P.op("dve", lambda e, o=tv: e.tensor_tensor(o, mu[:, :], mu[:, :], ALU.mult), reads=[("mu",)], writes=[tk])
        P.op("dve", lambda e, o=tv: e.tensor_tensor(o, ps[:, 7, :], o, ALU.subtract), reads=[qb, tk], writes=[tk])
        P.op("act", lambda e, a=tv: e.activation(rstd[:, :], a, AF.Sqrt, bias=EPS),
             reads=[tk], writes=[("rstd",)])
        P.op("dve", lambda e: e.reciprocal(rstd[:, :], rstd[:, :]), reads=[("rstd",)], writes=[("rstd",)])
        def ln_apply(c):
            t1, k1 = tmp_alloc(main)
            P.op("dve", lambda e, o=t1, a=R1[:, c, :]: e.tensor_tensor(o, a, mu[:, :], ALU.subtract),
                 reads=[("R1", c), ("mu",)], writes=[k1])
            P.op("dve", lambda e, o=t1: e.tensor_tensor(o, o, rstd[:, :], ALU.mult),
                 reads=[k1, ("rstd",)], writes=[k1])
            P.op("act", lambda e, o=R2[:, c, :], a=t1, s=vecs[:, V_CG, c:c + 1], bb=vecs[:, V_CB, c:c + 1]:
                 e.activation(o, a, AF.Silu, bias=bb, scale=s),
                 reads=[k1, ("vecs",)], writes=[("R2", c)])
        for side in (0, 1):
            gcol = (4 + side) * D
            wp = "wa" if side == 0 else "wb"
            src = ya if side == 0 else R2
            srck = (lambda k: ("ya", k)) if side == 0 else (lambda k: ("R2", k))
            for dq in range(4):
                wtg, wkg = wtile("win", 0, NDC, gcol + dq * 512, gcol + (dq + 1) * 512)
                wtp, wkp = wtile(wp, 0, NDC, dq * 512, (dq + 1) * 512)
                sg = []
                if side == 0:
                    for c in range(dq * 4, dq * 4 + 4):
                        ln_apply(c)
                for cl in range(4):
                    gv, gk = pbank(main)
                    mm_group(gv, [(wtg[:, k, cl * 128:(cl + 1) * 128], xn[:, k, :]) for k in range(NDC)],
                             reads=[wkg] + [("xn", k) for k in range(NDC)], writes=[gk])
                    tv, tk = tmp_alloc(main)
                    P.op("act", lambda e, o=tv, a=gv: e.activation(o, a, AF.Sigmoid), reads=[gk], writes=[tk])
                    sg.append((tv, tk))
                for cl in range(4):
                    dm = dq * 4 + cl
                    tv, tk = sg[cl]
                    pv, pk = pbank(main)
                    mm_group(pv, [(wtp[:, k, cl * 128:(cl + 1) * 128], src[:, k, :]) for k in range(NDC)],
                             reads=[wkp] + [srck(k) for k in range(NDC)], writes=[pk])
                    if side == 0:
                        P.op("dve", lambda e, o=R1[:, dm, :], a=tv, b=pv: e.tensor_tensor(o, a, b, ALU.mult),
                             reads=[tk, pk], writes=[("R1", dm)])
                    else:
                        P.op("dve", lambda e, o=tv, b=pv: e.tensor_tensor(o, o, b, ALU.mult),
                             reads=[tk, pk], writes=[tk])
                        P.op("dve", lambda e, o=ya[:, dm, :], a=tv, b=R1[:, dm, :]: e.tensor_tensor(o, a, b, ALU.add),
                             reads=[tk, ("R1", dm)], writes=[("ya", dm)])
        for dq in range(4):
            wt, wk = wtile("wo", 0, NDC, dq * 512, (dq + 1) * 512)
            for cl in range(4):
                dc = dq * 4 + cl
                bv, bk = pbank(main)
                mm_group(bv, [(wt[:, k, cl * 128:(cl + 1) * 128], ya[:, k, :]) for k in range(NDC)],
                         reads=[wk] + [("ya", k) for k in range(NDC)], writes=[bk])
                P.op("dve", lambda e, o=h[:, dc, :], b=bv: e.tensor_tensor(o, o, b, ALU.add),
                     reads=[bk, ("h", dc)], writes=[("h", dc)])

    def ple_and_out(p):
        t0 = p * TN
        rmsnorm_raw(main, V_PLE)
        P.op("pool", lambda e, t0=t0: e.dma_start(out=pTb[:, :, :], in_=pTv[:, :, t0:t0 + TN]),
             writes=[("pTb",)], sem="pld", inc=16)
        def pre_fin(dc):
            P.op("dve", lambda e, o=R1[:, dc, :], a=h[:, dc, :], s=vecs[:, V_FIN, dc:dc + 1]:
                 e.tensor_scalar(o, a, s, None, ALU.mult),
                 reads=[("h", dc), ("vecs",)], writes=[("R1", dc)])
        for dq in range(4):
            wt, wk = wtile("wpg", 0, NDC, dq * 512, (dq + 1) * 512)
            wpp_t, wpp_k = wtile("wpp", 0, 2, dq * 512, (dq + 1) * 512)
            sg = []
            for cl in range(4):
                gv, gk = pbank(main)
                if dq == 0 and cl == 0:
                    mm_group_split(gv, [(wt[:, k, cl * 128:(cl + 1) * 128], xn[:, k, :]) for k in range(NDC)],
                                   [("xn", k) for k in range(NDC)], [wk], [gk])
                else:
                    mm_group(gv, [(wt[:, k, cl * 128:(cl + 1) * 128], xn[:, k, :]) for k in range(NDC)],
                             reads=[wk] + [("xn", k) for k in range(NDC)], writes=[gk])
                tv, tk = tmp_alloc(main)
                P.op("dve", lambda e, o=tv, a=gv: e.tensor_tensor(o, a, rstd[:, :], ALU.mult),
                     reads=[gk, ("rstd",)], writes=[tk])
                P.op("act", lambda e, o=tv: e.activation(o, o, AF.Sigmoid), reads=[tk], writes=[tk])
                sg.append((tv, tk))
                if dq >= 1 and cl == 1:
                    for dc in range(4 * dq - 4, 4 * dq):
                        rms_stat_chunk(main, main.h, main.hk, dc, pre_fin)
            for cl in range(4):
                dc = dq * 4 + cl
                tv, tk = sg[cl]
                pv, pk = pbank(main)
                mm_group(pv, [(wpp_t[:, k, cl * 128:(cl + 1) * 128], pTb[:, k, :]) for k in range(2)],
                         reads=[wpp_k, ("pTb",)], writes=[pk])
                P.op("dve", lambda e, o=tv, b=pv: e.tensor_tensor(o, o, b, ALU.mult),
                     reads=[tk, pk], writes=[tk])
                P.op("dve", lambda e, o=h[:, dc, :], a=tv: e.tensor_tensor(o, o, a, ALU.add),
                     reads=[tk, ("h", dc)], writes=[("h", dc)])
        for dc in range(12, 16):
            rms_stat_chunk(main, main.h, main.hk, dc, pre_fin)
        rms_finish(main, dest=mu[:, :], destk=("mu",))
        if p == 0:
            load_x(1)
        return lambda: emit_out(t0)

    def emit_out(t0):
        for dc in range(NDC):
            oi = rot("o", 4)
            ov, ok = tmp[:, oi, :], ("tmp", oi)
            P.op("dve", lambda e, o=ov, a=R1[:, dc, :]: e.tensor_tensor(o, a, mu[:, :], ALU.mult),
                 reads=[("R1", dc), ("mu",)], writes=[ok])
            P.op("sp", lambda e, o=oTv[:, dc, t0:t0 + TN], a=ov: e.dma_start(out=o, in_=a),
                 reads=[ok], sem="o%d" % oi, inc=16)

    cdma = []

    def cload(eng, dst, src, key):
        P.op(eng, lambda e: e.dma_start(out=dst, in_=src), writes=[key], sem="cst", inc=16)
        cdma.append(key)
    cload("sp", vecs[:, :, :], vecs_d.rearrange("p (v k) -> p v k", k=NDC), ("vecs",))
    cload("sp", dww[:, :, :], dww_d.rearrange("p (c k) -> p c k", k=CW), ("dww",))
    cload("sp", bsB, bsB_d.rearrange("p (g i) -> p g i", i=128), ("bsB",))
    cload("sp", hmask[:, :], hmask_d, ("hmask",))
    cload("sp", hh[:, :, :], xhv, ("hhall",))
    P.op("pool", lambda e: e.dma_start(out=wmT[:, :, :], in_=wsT_d.rearrange("p (g i) -> p g i", i=128)),
         writes=[("wmT",)], sem="cst2", inc=16)
    for k in cdma:
        P.lastw[k] = ("cst", P.cnt["cst"])
    for dc in range(NDC):
        P.lastw[("hh", dc)] = ("cst", P.cnt["cst"])
    P.op("dve", lambda e: e.memset(ones1[:, :], 1.0), writes=[("ones1",)])
    P.op("dve", lambda e: e.memset(onesD[:, :], 1.0 / D), writes=[("onesD",)])
    P.op("dve", lambda e: e.memset(ident[:, :], 1.0), writes=[("ident",)])
    P.op("pool", lambda e: e.affine_select(ident[:, :], ident[:, :], [[-1, 128]], ALU.is_equal, 0.0,
                                           base=0, channel_multiplier=1),
         reads=[("ident",)], writes=[("ident",)])
    P.op("dve", lambda e: e.memset(wmT[64:128, :, 0:64], 0.0), reads=[("wmT",)], writes=[("wmT",)])
    for half in range(2):
        b = bank()
        for gl in range(4):
            g = half * 4 + gl
            mm_group(ps[:, b, gl * 128:(gl + 1) * 128], [(ones1[:, :], wmT[:, g, :])],
                     reads=[("ones1",), ("wmT",)], writes=[("ps", b)], force_start=(gl == 0))
        for gl in range(4):
            g = half * 4 + gl
            for c in (2 * g, 2 * g + 1):
                P.op("dve", lambda e, o=Cc[:, c, :], a=ps[:, b, gl * 128:(gl + 1) * 128], s=vecs[:, V_SB, c:c + 1],
                     bb=bsB[:, g, :]: e.scalar_tensor_tensor(o, a, s, bb, ALU.mult, ALU.add),
                     reads=[("ps", b), ("vecs",), ("bsB",)], writes=[("Cc",)])
    for i in range(2):
        P.lastw[("R1", i)] = P.lastw[("Cc",)]

    def load_x(p):
        t0 = p * TN
        for q in range(4):
            P.op("sp", lambda e, t0=t0, q=q: e.dma_start(out=h[:, 4 * q:4 * q + 4, :], in_=xTv[:, 4 * q:4 * q + 4, t0:t0 + TN]),
                 writes=[("h", dc) for dc in range(4 * q, 4 * q + 4)], sem="xld%d" % q, inc=16)

    load_x(0)
    pending_out = None
    for p in range(2):
        ffn([main, halo] if p == 0 else [main], V_F1, "f1g", "f1u", "f1d", after_norm=pending_out)
        mix(p)
        ffn([main], V_F2, "f2g", "f2u", "f2d")
        pending_out = ple_and_out(p)
    pending_out()
    P.wait_all("sp", ["o0", "o1", "o2", "o3"])

    sem_names = sorted(P.cnt.keys())
    sems = {s: st.enter_context(nc.semaphore(s)) for s in sem_names}

    def run(name, e):
        for waits, fn, s, inc in P.ops[name]:
            for ws, wv in waits:
                e.wait_ge(sems[ws], wv)
            if fn is not None:
                fn(e).then_inc(sems[s], inc)

    with nc.Block() as block:
        @block.tensor
        def _(e):
            run("pe", e)

        @block.scalar
        def _(e):
            run("act", e)

        @block.vector
        def _(e):
            run("dve", e)

        @block.gpsimd
        def _(e):
            run("pool", e)

        @block.sync
        def _(e):
            run("sp", e)
    st.close()
    return nc


_NC = None


def _prep_inputs(x, p, ffn1_norm, ffn1_w_gate, ffn1_w_up, ffn1_w_down, mix_norm, w_in,
                 sgu_ln_g, sgu_ln_b, sgu_w, sgu_b, w_a_proj, dw_w, dw_b, conv_ln_g,
                 conv_ln_b, w_b_proj, w_out, ffn2_norm, ffn2_w_gate, ffn2_w_up,
                 ffn2_w_down, ple_norm, w_ple_gate, w_ple_proj, final_norm):
    f = lambda a: np.ascontiguousarray(np.asarray(a, dtype=np.float32))
    x2 = f(x)[0]
    p2 = f(p)[0, 0]
    vl = [ffn1_norm[0], mix_norm[0], sgu_ln_g[0], sgu_ln_b[0], dw_b[0], conv_ln_g[0], conv_ln_b[0],
          ffn2_norm[0], ple_norm[0], final_norm]
    vecs = np.stack([f(v).reshape(NDC, 128).T for v in vl], axis=1)
    vecs = f(vecs.reshape(128, NV * NDC))
    dww = f(f(dw_w)[0].T.reshape(NDC, 128, CW).transpose(1, 0, 2).reshape(128, NDC * CW))
    wsT = f(f(sgu_w)[0].transpose(2, 0, 1).reshape(128, 8 * 128))
    bsB = f(np.broadcast_to(f(sgu_b)[0].reshape(1, 8 * 128), (128, 8 * 128)))
    shared = {
        "vecs": vecs, "dww": dww, "wsT": wsT, "bsB": bsB,
        "f1g": f(ffn1_w_gate)[0], "f1u": f(ffn1_w_up)[0], "f1d": f(ffn1_w_down)[0],
        "win": f(w_in)[0], "wa": f(w_a_proj)[0], "wb": f(w_b_proj)[0], "wo": f(w_out)[0],
        "f2g": f(ffn2_w_gate)[0], "f2u": f(ffn2_w_up)[0], "f2d": f(ffn2_w_down)[0],
        "wpg": f(w_ple_gate)[0], "wpp": f(w_ple_proj)[0],
    }
    in_maps = []
    for c in range(NCORES):
        s0 = c * T
        m = dict(shared)
        m["xT"] = f(x2[s0:s0 + T].T)
        m["pT"] = f(p2[s0:s0 + T].T)
        if c == 0:
            m["xh"] = np.zeros((D, HN), np.float32)
            m["hmask"] = np.zeros((128, 1), np.float32)
        else:
            m["xh"] = f(x2[s0 - HN:s0].T)
            m["hmask"] = np.ones((128, 1), np.float32)
        in_maps.append(m)
    return in_maps


def kernel(**inputs):
    global _NC
    in_maps = _prep_inputs(**inputs)
    if _NC is None:
        _NC = build_nc()
    res = run_bass_kernel_spmd(_NC, in_maps, core_ids=list(range(NCORES)))
    out = np.concatenate([np.asarray(r["oT"], dtype=np.float32).T for r in res.results], axis=0)
    return out.reshape(1, NCORES * T, D)
```
